# Optimizing a Trainium2 kernel written in Bass

```python
import math
import numpy as np
import jax
import jax.numpy as jnp
from jax import lax

D_MODEL = 2048
BATCH = 2
SEQ = 16384
DEPTH = 1

CTX_LEN = 256
GRID_W = 64

MLA_HEADS = 8
MLA_Q_RANK = 512
MLA_KV_RANK = 512
MLA_NOPE = 128
MLA_ROPE = 64
MLA_V = 128
MLA_QK = MLA_NOPE + MLA_ROPE
ROPE_BASE = 10000.0
Q_BLOCK = 128

DN_HEADS = 8
DN_DK = 128
DN_DV = 128
DN_CONV = 3
DN_CHUNK = 64

FFN_HIDDEN = 5504
FFN_CONV = 3

EPS = 1e-6

IN_SPLITS = (MLA_Q_RANK, MLA_KV_RANK, MLA_ROPE,
             DN_HEADS * DN_DK, DN_HEADS * DN_DK, DN_HEADS * DN_DV, DN_HEADS * DN_DV,
             2 * DN_HEADS, 2 * DN_HEADS, D_MODEL, D_MODEL)
IN_WIDTH = sum(IN_SPLITS)

kernel_name = 'hybrid_mla_gdn_convglu_dit'


def rmsnorm(x, w):
    xf = x.astype(jnp.float32)
    y = xf * lax.rsqrt(jnp.mean(xf * xf, axis=-1, keepdims=True) + EPS)
    return (y * w.astype(jnp.float32)).astype(x.dtype)


def l2norm(x):
    xf = x.astype(jnp.float32)
    return xf * lax.rsqrt(jnp.sum(xf * xf, axis=-1, keepdims=True) + EPS)


def modulate(x, shift, scale):
    return x * (1 + jnp.expand_dims(scale, -2)) + jnp.expand_dims(shift, -2)


def dwconv_centred(x, w):
    k = w.shape[0]
    p = k // 2
    t = x.shape[1]
    xp = jnp.pad(x, ((0, 0), (p, p), (0, 0)))
    return sum(xp[:, j:j + t] * w[j] for j in range(k))


def split_columns(p):
    offs = np.cumsum(np.array(IN_SPLITS))[:-1].tolist()
    return jnp.split(p, offs, axis=-1)


def axial_rope_tables(rows, dtype):
    n = MLA_ROPE // 4
    row = jnp.repeat(jnp.arange(rows, dtype=jnp.float32), GRID_W)
    col = jnp.tile(jnp.arange(GRID_W, dtype=jnp.float32), rows)
    inv = ROPE_BASE ** (-jnp.arange(n, dtype=jnp.float32) / n)
    ang_r = row[:, None, None] * inv
    ang_c = col[:, None, None] * inv
    return tuple(a.astype(dtype) for a in (jnp.cos(ang_r), jnp.sin(ang_r), jnp.cos(ang_c), jnp.sin(ang_c)))


def rotate_pairs(x, cos, sin):
    a, b = jnp.split(x, 2, axis=-1)
    return jnp.concatenate([a * cos - b * sin, b * cos + a * sin], axis=-1)


def apply_axial_rope(t, tabs):
    cos_r, sin_r, cos_c, sin_c = tabs
    nope, rr, rc = jnp.split(t, [MLA_NOPE, MLA_NOPE + MLA_ROPE // 2], axis=-1)
    return jnp.concatenate([nope, rotate_pairs(rr, cos_r, sin_r), rotate_pairs(rc, cos_c, sin_c)], axis=-1)


def mla_q(q_a, q_a_norm, w_q_b, q_norm, tabs):
    b, t, _ = q_a.shape
    q = (rmsnorm(q_a, q_a_norm) @ w_q_b).reshape(b, t, MLA_HEADS, MLA_QK)
    q = rmsnorm(q, q_norm)
    return q if tabs is None else apply_axial_rope(q, tabs)


def mla_kv(kv_a, k_rope, kv_a_norm, w_kv_b, k_norm, tabs):
    b, t, _ = kv_a.shape
    kv = (rmsnorm(kv_a, kv_a_norm) @ w_kv_b).reshape(b, t, MLA_HEADS, MLA_NOPE + MLA_V)
    k_nope, v = jnp.split(kv, [MLA_NOPE], axis=-1)
    k_r = jnp.broadcast_to(k_rope[:, :, None, :], (b, t, MLA_HEADS, MLA_ROPE))
    k = rmsnorm(jnp.concatenate([k_nope, k_r], axis=-1), k_norm)
    k = k if tabs is None else apply_axial_rope(k, tabs)
    return k, v


def attend_dense(q, k, v):
    b, n, h, _ = q.shape
    s = jnp.einsum('bqhd,bkhd->bhqk', q, k).astype(jnp.float32) * (MLA_QK ** -0.5)
    p = jax.nn.softmax(s, axis=-1).astype(v.dtype)
    return jnp.einsum('bhqk,bkhd->bqhd', p, v).reshape(b, n, h * MLA_V)


def attend_latent(q, k_lat, v_lat, k_ctx, v_ctx):
    b, t, h, dq = q.shape
    nb = t // Q_BLOCK
    qb = q.reshape(b, nb, Q_BLOCK, h, dq).transpose(1, 0, 2, 3, 4)

    def block(qi):
        s = jnp.concatenate([jnp.einsum('bqhd,bkhd->bhqk', qi, k_lat),
                             jnp.einsum('bqhd,bkhd->bhqk', qi, k_ctx)], axis=-1)
        p = jax.nn.softmax(s.astype(jnp.float32) * (MLA_QK ** -0.5), axis=-1).astype(v_lat.dtype)
        return (jnp.einsum('bhqk,bkhd->bqhd', p[..., :t], v_lat)
                + jnp.einsum('bhqk,bkhd->bqhd', p[..., t:], v_ctx))

    o = lax.map(block, qb)
    return o.transpose(1, 0, 2, 3, 4).reshape(b, t, h * MLA_V)


def dn_prepare(dq, dk, dv, da, db, conv_w, a_log, dt_bias, need_q):
    b, t, _ = dk.shape
    wq, wk, wv = jnp.split(conv_w, [DN_HEADS * DN_DK, 2 * DN_HEADS * DN_DK], axis=-1)

    def conv_heads(a, w, d):
        return jax.nn.silu(dwconv_centred(a, w)).reshape(b, t, DN_HEADS, d).transpose(0, 2, 1, 3)

    k = l2norm(conv_heads(dk, wk, DN_DK))
    v = conv_heads(dv, wv, DN_DV).astype(jnp.float32)
    q = l2norm(conv_heads(dq, wq, DN_DK)) * (DN_DK ** -0.5) if need_q else None
    g = -jnp.exp(a_log.astype(jnp.float32)) * jax.nn.softplus(
        da.reshape(b, t, 2, DN_HEADS).astype(jnp.float32) + dt_bias.astype(jnp.float32))
    beta = jax.nn.sigmoid(db.reshape(b, t, 2, DN_HEADS).astype(jnp.float32))
    return q, k, v, g.transpose(2, 0, 3, 1), beta.transpose(2, 0, 3, 1)


def gated_delta_chunked(q, k, v, g, beta, s0, with_out):
    b, h, t, dk = k.shape
    dv = v.shape[-1]
    c = DN_CHUNK
    nc = t // c
    k = k.reshape(b, h, nc, c, dk)
    v = v.reshape(b, h, nc, c, dv)
    gc = jnp.cumsum(g.reshape(b, h, nc, c), axis=-1)
    beta = beta.reshape(b, h, nc, c, 1)
    incl = jnp.tril(jnp.ones((c, c), dtype=bool))
    strict = jnp.tril(jnp.ones((c, c), dtype=bool), -1)
    decay = jnp.exp(jnp.where(incl, gc[..., :, None] - gc[..., None, :], -jnp.inf))
    kb = k * beta
    m = jnp.where(strict, jnp.einsum('bhnid,bhnjd->bhnij', kb, k) * decay, 0.0)
    rhs = jnp.concatenate([v * beta, kb * jnp.exp(gc)[..., None]], axis=-1)
    sol = lax.linalg.triangular_solve(m + jnp.eye(c, dtype=m.dtype), rhs,
                                      left_side=True, lower=True, unit_diagonal=True)
    u, w = sol[..., :dv], sol[..., dv:]
    k_end = k * jnp.exp(gc[..., -1:] - gc)[..., None]
    g_end = jnp.exp(gc[..., -1])[..., None, None]
    xs = [u, w, k_end, g_end]
    if with_out:
        q = q.reshape(b, h, nc, c, dk)
        xs += [q * jnp.exp(gc)[..., None], jnp.einsum('bhnid,bhnjd->bhnij', q, k) * decay]
    xs = [jnp.moveaxis(a, 2, 0) for a in xs]

    def step(s, inp):
        u_c, w_c, k_c, g_c = inp[:4]
        v_new = u_c - jnp.einsum('bhcd,bhde->bhce', w_c, s)
        s_next = s * g_c + jnp.einsum('bhcd,bhce->bhde', k_c, v_new)
        if not with_out:
            return s_next, None
        q_c, a_c = inp[4:]
        o = jnp.einsum('bhcd,bhde->bhce', q_c, s) + jnp.einsum('bhcj,bhje->bhce', a_c, v_new)
        return s_next, o

    s_final, o = lax.scan(step, s0, xs)
    if not with_out:
        return None, s_final
    return jnp.moveaxis(o, 0, 2).reshape(b, h, t, dv), s_final


def dn_bidirectional(q, k, v, g, beta, s0, with_out):
    def flip(a):
        return None if a is None else jnp.flip(a, axis=2)

    o_f, s_f = gated_delta_chunked(q, k, v, g[0], beta[0], s0[0], with_out)
    o_b, s_b = gated_delta_chunked(flip(q), flip(k), flip(v), flip(g[1]), flip(beta[1]), s0[1], with_out)
    states = jnp.stack([s_f, s_b])
    if not with_out:
        return None, states
    return o_f + flip(o_b), states


def dn_output(o, z, o_norm, dtype):
    b, h, t, dv = o.shape
    y = rmsnorm(o.transpose(0, 2, 1, 3), o_norm) * jax.nn.silu(z.reshape(b, t, h, dv).astype(jnp.float32))
    return y.reshape(b, t, h * dv).astype(dtype)


def merge_branches(y_mla, y_dn, gate_mla, gate_dn, w_o_mla, w_o_dn, w_out):
    merged = jax.nn.sigmoid(gate_mla) * (y_mla @ w_o_mla) + jax.nn.sigmoid(gate_dn) * (y_dn @ w_o_dn)
    return merged @ w_out


def conv_glu(h, w_up, conv_w, w_down):
    gate, val = jnp.split(h @ w_up, 2, axis=-1)
    return (jax.nn.silu(dwconv_centred(gate, conv_w)) * val) @ w_down


def setup_inputs(seed: int = 0) -> dict:
    key = jax.random.key(seed)
    ks = jax.random.split(key, 32)
    L, D, H = DEPTH, D_MODEL, DN_HEADS
    f32 = jnp.float32

    def nrm(i, shape, scale):
        return jax.random.normal(ks[i], shape, f32) * scale

    def gain(i, n):
        return 1.0 + 0.01 * jax.random.normal(ks[i], (L, n), f32)

    dt = jnp.exp(jax.random.uniform(ks[20], (L, 2, H), f32, math.log(1e-3), math.log(1e-1)))
    dt_bias = dt + jnp.log(-jnp.expm1(-dt))
    a_log = jnp.log(jax.random.uniform(ks[21], (L, 2, H), f32, 1.0, 16.0))
    return {
        'x': nrm(0, (BATCH, SEQ, D), 1.0),
        'c': nrm(1, (BATCH, D), 1.0),
        'ctx': nrm(2, (BATCH, CTX_LEN, D), 1.0),
        'c_ctx': nrm(3, (D,), 1.0),
        'w_ada': nrm(4, (L, D, 6 * D), 0.5 * D ** -0.5),
        'b_ada': nrm(5, (L, 6 * D), 0.01),
        'norm_mix': gain(6, D),
        'w_in': nrm(7, (L, D, IN_WIDTH), D ** -0.5),
        'q_a_norm': gain(8, MLA_Q_RANK),
        'w_q_b': nrm(9, (L, MLA_Q_RANK, MLA_HEADS * MLA_QK), MLA_Q_RANK ** -0.5),
        'kv_a_norm': gain(10, MLA_KV_RANK),
        'w_kv_b': nrm(11, (L, MLA_KV_RANK, MLA_HEADS * (MLA_NOPE + MLA_V)), MLA_KV_RANK ** -0.5),
        'q_norm': gain(12, MLA_QK),
        'k_norm': gain(13, MLA_QK),
        'w_o_mla': nrm(14, (L, MLA_HEADS * MLA_V, D), (MLA_HEADS * MLA_V) ** -0.5),
        'dn_conv': nrm(15, (L, DN_CONV, DN_HEADS * (2 * DN_DK + DN_DV)), DN_CONV ** -0.5),
        'dn_a_log': a_log,
        'dn_dt_bias': dt_bias,
        'dn_o_norm': gain(16, DN_DV),
        'w_o_dn': nrm(17, (L, DN_HEADS * DN_DV, D), (DN_HEADS * DN_DV) ** -0.5),
        'w_out': nrm(18, (L, D, D), D ** -0.5),
        'norm_ffn': gain(19, D),
        'w_ffn_up': nrm(22, (L, D, 2 * FFN_HIDDEN), D ** -0.5),
        'ffn_conv': nrm(23, (L, FFN_CONV, FFN_HIDDEN), FFN_CONV ** -0.5),
        'w_ffn_down': nrm(24, (L, FFN_HIDDEN, D), FFN_HIDDEN ** -0.5),
    }


def reference(x, c, ctx, c_ctx, w_ada, b_ada, norm_mix, w_in, q_a_norm, w_q_b, kv_a_norm, w_kv_b,
              q_norm, k_norm, w_o_mla, dn_conv, dn_a_log, dn_dt_bias, dn_o_norm, w_o_dn, w_out,
              norm_ffn, w_ffn_up, ffn_conv, w_ffn_down):
    b, t, _ = x.shape
    rows = t // GRID_W
    tabs = axial_rope_tables(rows, x.dtype)
    s_zero = jnp.zeros((2, b, DN_HEADS, DN_DK, DN_DV), jnp.float32)
    for l in range(DEPTH):
        last = l == DEPTH - 1
        sh1, sc1, g1, sh2, sc2, g2 = jnp.split(jax.nn.silu(c) @ w_ada[l] + b_ada[l], 6, axis=-1)
        csh1, csc1, cg1, csh2, csc2, cg2 = jnp.split(jax.nn.silu(c_ctx) @ w_ada[l] + b_ada[l], 6, axis=-1)

        (lq_a, lkv_a, lk_r, ldq, ldk, ldv, ldz, lda, ldb, lga, lgb) = split_columns(
            modulate(rmsnorm(x, norm_mix[l]), sh1, sc1) @ w_in[l])
        (cq_a, ckv_a, ck_r, cdq, cdk, cdv, cdz, cda, cdb, cga, cgb) = split_columns(
            modulate(rmsnorm(ctx, norm_mix[l]), csh1, csc1) @ w_in[l])

        k_c, v_c = mla_kv(ckv_a, ck_r, kv_a_norm[l], w_kv_b[l], k_norm[l], None)
        k_l, v_l = mla_kv(lkv_a, lk_r, kv_a_norm[l], w_kv_b[l], k_norm[l], tabs)
        q_l = mla_q(lq_a, q_a_norm[l], w_q_b[l], q_norm[l], tabs)
        y_mla_l = attend_latent(q_l, k_l, v_l, k_c, v_c)

        cq, ck, cv, cgd, cbeta = dn_prepare(cdq, cdk, cdv, cda, cdb, dn_conv[l], dn_a_log[l], dn_dt_bias[l], not last)
        o_dn_c, s_ctx = dn_bidirectional(cq, ck, cv, cgd, cbeta, s_zero, not last)
        lq, lk, lv, lgd, lbeta = dn_prepare(ldq, ldk, ldv, lda, ldb, dn_conv[l], dn_a_log[l], dn_dt_bias[l], True)
        o_dn_l, _ = dn_bidirectional(lq, lk, lv, lgd, lbeta, s_ctx, True)

        y_l = merge_branches(y_mla_l, dn_output(o_dn_l, ldz, dn_o_norm[l], x.dtype), lga, lgb,
                             w_o_mla[l], w_o_dn[l], w_out[l])

        if not last:
            q_c = mla_q(cq_a, q_a_norm[l], w_q_b[l], q_norm[l], None)
            y_c = merge_branches(attend_dense(q_c, k_c, v_c), dn_output(o_dn_c, cdz, dn_o_norm[l], ctx.dtype),
                                 cga, cgb, w_o_mla[l], w_o_dn[l], w_out[l])
            ctx = ctx + jnp.expand_dims(cg1, -2) * y_c
            ctx = ctx + jnp.expand_dims(cg2, -2) * conv_glu(
                modulate(rmsnorm(ctx, norm_ffn[l]), csh2, csc2), w_ffn_up[l], ffn_conv[l], w_ffn_down[l])

        x = x + jnp.expand_dims(g1, -2) * y_l

        x = x + jnp.expand_dims(g2, -2) * conv_glu(
            modulate(rmsnorm(x, norm_ffn[l]), sh2, sc2), w_ffn_up[l], ffn_conv[l], w_ffn_down[l])
    return x
```

```python
import contextlib
import math
import numpy as np
import ml_dtypes
import concourse.bass as bass
import concourse.mybir as mybir
from concourse.bass_utils import run_bass_kernel_spmd

F32 = mybir.dt.float32
BF16 = mybir.dt.bfloat16
AF = mybir.ActivationFunctionType
ALU = mybir.AluOpType
AX = mybir.AxisListType

ENGS = ("pe", "act", "dve", "pool", "sp")

D = 2048
T = 16384
TC = 256
NTOK = T + TC
H = 8
FH = 5504
EPS = 1e-6
INW = 9312
NCORES = 2


class Buf:
    __slots__ = ("t", "name", "lw", "rd", "track")

    def __init__(self, t, name, track=True):
        self.t = t
        self.name = name
        self.lw = None
        self.rd = {}
        self.track = track

    def __getitem__(self, idx):
        return V(self, self.t[idx])


class V:
    __slots__ = ("b", "ap")

    def __init__(self, b, ap):
        self.b = b
        self.ap = ap

    def __getitem__(self, idx):
        return V(self.b, self.ap[idx])

    def rearrange(self, s, **kw):
        return V(self.b, self.ap.rearrange(s, **kw))

    def unsqueeze(self, a):
        return V(self.b, self.ap.unsqueeze(a))

    def bc(self, shape):
        return V(self.b, self.ap.to_broadcast(list(shape)))


class Sched:
    def __init__(self, nc, semstack, n_dma_sems=16):
        self.nc = nc
        self.q = {e: [] for e in ENGS}
        self.seq = {e: 0 for e in ENGS}
        self.waited = {e: {} for e in ENGS}
        self.n_dma_sems = n_dma_sems
        self.dma_cnt = {}
        self.dma_rr = {e: 0 for e in ENGS}
        self.sems = {}
        self.semstack = semstack
        self.ninstr = 0

    def _sem(self, k):
        if k not in self.sems:
            nm = "s_" + "_".join(str(x) for x in k)
            self.sems[k] = self.semstack.enter_context(self.nc.semaphore(nm))
        return self.sems[k]

    def _need(self, eng, toks):
        best = {}
        for tk in toks:
            if tk is None:
                continue
            k, v = tk
            if k == ("eng", "pe") and eng == "pe":
                continue
            if best.get(k, 0) < v:
                best[k] = v
        for k, v in best.items():
            if self.waited[eng].get(k, 0) >= v:
                continue
            self.waited[eng][k] = v
            self.q[eng].append(("wait", k, v))
            self.ninstr += 1

    def _deps(self, reads, writes):
        toks = []
        for b in reads:
            if b.track:
                toks.append(b.lw)
        for b in writes:
            if b.track:
                toks.append(b.lw)
                toks.extend(b.rd.items())
        return toks

    def _commit(self, tok, reads, writes):
        for b in reads:
            if b.track:
                if b.rd.get(tok[0], 0) < tok[1]:
                    b.rd[tok[0]] = tok[1]
        for b in writes:
            if b.track:
                b.lw = tok
                b.rd = {}

    def op(self, eng, fn, reads=(), writes=()):
        self._need(eng, self._deps(reads, writes))
        self.seq[eng] += 1
        tok = (("eng", eng), self.seq[eng])
        self.q[eng].append(("op", fn, ("eng", eng)))
        self.ninstr += 1
        self._commit(tok, reads, writes)
        return tok

    def dma(self, queue, out, in_, **kw):
        reads, writes = [in_.b], [out.b]
        toks = self._deps(reads, writes)
        i = self.dma_rr[queue]
        self.dma_rr[queue] = (i + 1) % self.n_dma_sems
        key = ("dma", queue, i)
        prev = self.dma_cnt.get(key, 0)
        if prev:
            toks.append((key, prev))
        self._need(queue, toks)
        self.dma_cnt[key] = prev + 16
        tok = (key, prev + 16)
        oa, ia = out.ap, in_.ap

        def fn(e):
            return e.dma_start(out=oa, in_=ia, allow_slow_non_contiguous=True, **kw)
        self.q[queue].append(("dma", fn, key))
        self.ninstr += 1
        self._commit(tok, reads, writes)
        return tok

    def flush(self):
        toks = [(("eng", e), self.seq[e]) for e in ENGS if self.seq[e] > 0]
        toks += [(k, v) for k, v in self.dma_cnt.items() if v > 0]
        for e in ENGS:
            self._need(e, toks)
        qs = self.q
        self.q = {e: [] for e in ENGS}
        for lst in qs.values():
            for it in lst:
                self._sem(it[1] if it[0] == "wait" else it[2])
        sems = self.sems

        def run(e, lst):
            for it in lst:
                if it[0] == "wait":
                    e.wait_ge(sems[it[1]], it[2])
                elif it[0] == "op":
                    it[1](e).then_inc(sems[it[2]], 1)
                else:
                    it[1](e).then_inc(sems[it[2]], 16)

        with self.nc.Block() as block:
            @block.tensor
            def _(e):
                run(e, qs["pe"])

            @block.scalar
            def _(e):
                run(e, qs["act"])

            @block.vector
            def _(e):
                run(e, qs["dve"])

            @block.gpsimd
            def _(e):
                run(e, qs["pool"])

            @block.sync
            def _(e):
                run(e, qs["sp"])

    def mm(self, out, lhsT, rhs, start=True, stop=True):
        oa, la, ra = out.ap, lhsT.ap, rhs.ap
        return self.op("pe", lambda e: e.matmul(oa, lhsT=la, rhs=ra, start=start, stop=stop),
                       reads=[lhsT.b, rhs.b], writes=[out.b])

    def tr(self, out, in_, ident):
        oa, ia, da = out.ap, in_.ap, ident.ap
        return self.op("pe", lambda e: e.transpose(out=oa, in_=ia, identity=da),
                       reads=[in_.b, ident.b], writes=[out.b])

    def act(self, out, in_, func, bias=None, scale=None, accum=None, eng="act"):
        oa, ia = out.ap, in_.ap
        kw = {}
        rd = [in_.b]
        wr = [out.b]
        if bias is not None:
            if isinstance(bias, V):
                kw["bias"] = bias.ap
                rd.append(bias.b)
            else:
                kw["bias"] = bias
        if scale is not None:
            if isinstance(scale, V):
                kw["scale"] = scale.ap
                rd.append(scale.b)
            else:
                kw["scale"] = scale
        if accum is not None:
            kw["accum_out"] = accum.ap
            wr.append(accum.b)
        return self.op("act", lambda e: e.activation(out=oa, in_=ia, func=func, **kw), reads=rd, writes=wr)

    def tt(self, eng, out, in0, in1, op):
        oa, a, b = out.ap, in0.ap, in1.ap
        return self.op(eng, lambda e: e.tensor_tensor(out=oa, in0=a, in1=b, op=op),
                       reads=[in0.b, in1.b], writes=[out.b])

    def ts(self, eng, out, in0, s1, op0, s2=None, op1=None):
        oa, a = out.ap, in0.ap
        rd = [in0.b]
        if isinstance(s1, V):
            rd.append(s1.b)
            s1 = s1.ap
        if isinstance(s2, V):
            rd.append(s2.b)
            s2 = s2.ap
        if op1 is None:
            return self.op(eng, lambda e: e.tensor_scalar(out=oa, in0=a, scalar1=s1, scalar2=None, op0=op0),
                           reads=rd, writes=[out.b])
        return self.op(eng, lambda e: e.tensor_scalar(out=oa, in0=a, scalar1=s1, scalar2=s2, op0=op0, op1=op1),
                       reads=rd, writes=[out.b])

    def stt(self, out, in0, scalar, in1, op0, op1):
        oa, a, b = out.ap, in0.ap, in1.ap
        rd = [in0.b, in1.b]
        if isinstance(scalar, V):
            rd.append(scalar.b)
            scalar = scalar.ap
        return self.op("dve", lambda e: e.scalar_tensor_tensor(out=oa, in0=a, scalar=scalar, in1=b, op0=op0, op1=op1),
                       reads=rd, writes=[out.b])

    def cp(self, eng, out, in_):
        oa, ia = out.ap, in_.ap
        if eng == "act":
            return self.op("act", lambda e: e.activation(out=oa, in_=ia, func=AF.Copy), reads=[in_.b], writes=[out.b])
        return self.op(eng, lambda e: e.tensor_copy(out=oa, in_=ia), reads=[in_.b], writes=[out.b])

    def recip(self, out, in_):
        oa, ia = out.ap, in_.ap
        return self.op("dve", lambda e: e.reciprocal(out=oa, in_=ia), reads=[in_.b], writes=[out.b])

    def memset(self, eng, out, val):
        oa = out.ap
        return self.op(eng, lambda e: e.memset(oa, val), writes=[out.b])

    def reduce(self, out, in_, op=ALU.add):
        oa, ia = out.ap, in_.ap
        return self.op("dve", lambda e: e.tensor_reduce(out=oa, in_=ia, axis=AX.X, op=op), reads=[in_.b], writes=[out.b])


def build_program(dbg=False, stop_after=None):
    nc = bass.Bass("TRN2", target_bir_lowering=False)
    IN = {}

    def din(name, shape, dt=F32):
        IN[name] = Buf(nc.dram_tensor(name, list(shape), dt, kind="ExternalInput").ap(), name, track=False)
        return IN[name]

    x_d = din("x", [T, D])
    ctx_d = din("ctx", [TC, D])
    cc_d = din("cc", [D, 2])
    w_ada_d = din("w_ada", [D, 6 * D])
    b_ada_d = din("b_ada", [6 * D])
    norm_mix_d = din("norm_mix", [D])
    norm_ffn_d = din("norm_ffn", [D])
    w_in_d = din("w_in", [D, INW])
    qan_d = din("q_a_norm", [512])
    kvan_d = din("kv_a_norm", [512])
    w_qb_d = din("w_q_b", [512, 1536])
    w_kvb_d = din("w_kv_b", [512, 2048])
    qn_d = din("q_norm", [192])
    kn_d = din("k_norm", [192])
    w_omla_d = din("w_o_mla", [1024, D])
    dn_conv_d = din("dn_conv", [128, 24, 3])
    alog_d = din("a_log_rep", [128, 16])
    dtb_d = din("dt_bias_rep", [128, 16])
    onorm_d = din("dn_o_norm", [128])
    w_odn_d = din("w_o_dn", [1024, D])
    w_out_d = din("w_out", [D, D])
    w_up_d = din("w_ffn_up", [D, 2 * FH])
    fconv_d = din("ffn_conv", [128, 43, 3])
    w_dn_d = din("w_ffn_down", [FH, D])
    identf_d = din("identf", [128, 128])
    cos_d = din("cos_t", [64, NTOK])
    sin_d = din("sin_t", [64, NTOK])
    prot_d = din("protT", [64, 64])
    dnm_d = din("dn_masks", [64, 2, 4, 64])

    out_d = Buf(nc.dram_tensor("out", [T, D], F32, kind="ExternalOutput").ap(), "out", track=False)

    SCR = {}

    def scr(name, shape, dt):
        kind = "ExternalOutput" if (dbg and name in DBG_OUT) else "Internal"
        SCR[name] = Buf(nc.dram_tensor(name, list(shape), dt, kind=kind).ap(), name, track=False)
        return SCR[name]

    wb_in = scr("wb_in", [D, INW], BF16)
    wb_qb = scr("wb_qb", [512, 1536], BF16)
    wb_kvb = scr("wb_kvb", [512, 2048], BF16)
    wb_omla = scr("wb_omla", [1024, D], BF16)
    wb_odn = scr("wb_odn", [1024, D], BF16)
    wb_out = scr("wb_out", [D, D], BF16)
    wb_up = scr("wb_up", [D, 2 * FH], BF16)
    wb_dn = scr("wb_dn", [FH, D], BF16)
    rows_s = scr("rows", [8, D], F32)
    prel = scr("prel", [3072, T + 2], BF16)
    prec = scr("prec", [3072, TC + 2], BF16)
    zs_s = scr("zs", [T, 1024], BF16)
    gb_s = scr("gb", [NTOK, 32], F32)
    ga_s = scr("ga", [2 * D, T], BF16)
    kt_s = scr("kt", [H, 192, NTOK], BF16)
    vs_s = scr("vs", [NTOK, 1024], BF16)
    qt_s = scr("qt", [H, 192, T], BF16)
    dkt_s = scr("dkt", [1024, NTOK], BF16)
    dqt_s = scr("dqt", [1024, T], BF16)
    dktok_s = scr("dktok", [NTOK, 1024], BF16)
    dvtok_s = scr("dvtok", [NTOK, 1024], BF16)
    of_s = scr("of", [T, 1024], F32)
    yd_s = scr("yd", [1024, T], BF16)
    ym_s = scr("ym", [1024, T], BF16)
    x1_s = scr("x1", [T, D], F32)
    gp_s = scr("gp", [FH, T + 2], BF16)
    val_s = scr("val", [FH, T], BF16)

    with contextlib.ExitStack() as semstack:
        S = Sched(nc, semstack)

        class Phase:
            def __init__(self):
                self.st = contextlib.ExitStack()
                self.n = 0

            def sb(self, shape, dt, name=None):
                self.n += 1
                nm = name or f"t{id(self) % 100000}_{self.n}"
                return Buf(self.st.enter_context(nc.sbuf_tensor(nm, list(shape), dt)), nm)

            def ps(self, shape, dt, name=None):
                self.n += 1
                nm = name or f"p{id(self) % 100000}_{self.n}"
                return Buf(self.st.enter_context(nc.psum_tensor(nm, list(shape), dt)), nm)

            def end(self):
                S.flush()
                self.st.close()

        P = Phase()
        for (src, dst, rows, cols) in [(w_in_d, wb_in, D, INW), (w_qb_d, wb_qb, 512, 1536), (w_kvb_d, wb_kvb, 512, 2048),
                                       (w_omla_d, wb_omla, 1024, D), (w_odn_d, wb_odn, 1024, D), (w_out_d, wb_out, D, D),
                                       (w_up_d, wb_up, D, 2 * FH), (w_dn_d, wb_dn, FH, D)]:
            rb = 512 if rows % 512 == 0 else 688
            for r0 in range(0, rows, rb):
                S.dma("pool", dst[r0:r0 + rb, :], src[r0:r0 + rb, :], max_dma_last_dim=4096)
        zt = P.sb([128, 2], BF16)
        S.memset("dve", zt[:, :], 0.0)
        for (sc, nrow, ncol) in [(prel, 3072, T + 2), (prec, 3072, TC + 2), (gp_s, FH, T + 2)]:
            for r0 in range(0, nrow, 128):
                for c in (0, ncol - 1):
                    S.dma("sp", sc[r0:r0 + 128, c:c + 1], zt[:, 0:1])
        cs = P.sb([128, 16, 2], F32)
        S.dma("sp", cs[:, :, :], V(cc_d, cc_d.t.rearrange("(k p) c -> p k c", p=128)))
        S.act(cs[:, :, :], cs[:, :, :], AF.Silu)
        bad = P.sb([128, 96], F32)
        S.dma("sp", bad[:, :], V(b_ada_d, b_ada_d.t.rearrange("(o p) -> p o", p=128)))
        nmx = P.sb([128, 16], F32)
        nff = P.sb([128, 16], F32)
        S.dma("sp", nmx[:, :], V(norm_mix_d, norm_mix_d.t.rearrange("(o p) -> p o", p=128)))
        S.dma("sp", nff[:, :], V(norm_ffn_d, norm_ffn_d.t.rearrange("(o p) -> p o", p=128)))
        modT = P.sb([128, 96, 2], F32)
        wa = [P.sb([128, 16, 512], F32) for _ in range(2)]
        pa = [P.ps([128, 2], F32) for _ in range(2)]
        wa_v = V(w_ada_d, w_ada_d.t.rearrange("(k p) n -> p k n", p=128))
        for cb in range(24):
            w = wa[cb % 2]
            S.dma("sp", w[:, :, :], wa_v[:, :, cb * 512:(cb + 1) * 512])
            for o in range(4):
                oc = cb * 4 + o
                pp = pa[oc % 2]
                for k in range(16):
                    S.mm(pp[:, :], w[:, k, o * 128:(o + 1) * 128], cs[:, k, :], start=(k == 0), stop=(k == 15))
                S.ts("dve", modT[:, oc, :], pp[:, :], bad[:, oc:oc + 1], ALU.add)
        rw = P.sb([128, 8, 16], F32)
        S.stt(rw[:, 0, :], modT[:, 16:32, 0], 1.0, nmx[:, :], ALU.add, ALU.mult)
        S.cp("dve", rw[:, 1, :], modT[:, 0:16, 0])
        S.stt(rw[:, 2, :], modT[:, 16:32, 1], 1.0, nmx[:, :], ALU.add, ALU.mult)
        S.cp("dve", rw[:, 3, :], modT[:, 0:16, 1])
        S.cp("dve", rw[:, 4, :], modT[:, 32:48, 0])
        S.stt(rw[:, 5, :], modT[:, 64:80, 0], 1.0, nff[:, :], ALU.add, ALU.mult)
        S.cp("dve", rw[:, 6, :], modT[:, 48:64, 0])
        S.cp("dve", rw[:, 7, :], modT[:, 80:96, 0])
        for r in range(8):
            S.dma("sp", V(rows_s, rows_s.t[r, :].rearrange("(o p) -> p o", p=128)), rw[:, r, :])
        P.end()

        if stop_after == "0":
            return nc, S

        def load_row(Pz, r):
            b_ = Pz.sb([128, D], F32)
            S.dma("sp", b_[:, :], V(rows_s, rows_s.t[r, :].partition_broadcast(128)))
            return b_

        P = Phase()
        identf = P.sb([128, 128], F32)
        S.dma("sp", identf[:, :], identf_d[:, :])
        identb = P.sb([128, 128], BF16)
        S.cp("dve", identb[:, :], identf[:, :])
        onesb = P.sb([128, 128], BF16)
        S.memset("dve", onesb[:, :], 1.0)
        s1row = P.sb([128, D], F32)
        sh1row = P.sb([128, D], F32)
        S.dma("sp", s1row[:, :], V(rows_s, rows_s.t[2, :].partition_broadcast(128)))
        S.dma("sp", sh1row[:, :], V(rows_s, rows_s.t[3, :].partition_broadcast(128)))
        protT = P.sb([64, 64], F32)
        S.dma("sp", protT[:, :], prot_d[:, :])
        qanw = P.sb([128, 4], F32)
        kvanw = P.sb([128, 4], F32)
        S.dma("sp", qanw[:, :], V(qan_d, qan_d.t.rearrange("(o p) -> p o", p=128)))
        S.dma("sp", kvanw[:, :], V(kvan_d, kvan_d.t.rearrange("(o p) -> p o", p=128)))
        qnw = P.sb([128, 2], F32)
        knw = P.sb([128, 2], F32)
        for (dst, src) in ((qnw, qn_d), (knw, kn_d)):
            S.dma("sp", dst[:, 0:1], V(src, src.t[0:128].rearrange("(p o) -> p o", o=1)))
            S.dma("sp", dst[0:64, 1:2], V(src, src.t[128:192].rearrange("(p o) -> p o", o=1)))
        QSC = 192.0 ** -0.5
        alog = P.sb([128, 16], F32)
        dtb = P.sb([128, 16], F32)
        S.dma("sp", alog[:, :], alog_d[:, :])
        S.dma("sp", dtb[:, :], dtb_d[:, :])
        nexpa = P.sb([128, 16], F32)
        S.act(nexpa[:, :], alog[:, :], AF.Exp)
        S.ts("dve", nexpa[:, :], nexpa[:, :], -1.0, ALU.mult)
        wkn = P.sb([128, 4, 8, 128], BF16)
        wv = P.sb([128, 4, 8, 128], BF16)
        wqn = P.sb([128, 4, 8, 128], BF16)
        wqr = P.sb([128, 4, 8, 64], BF16)
        kvb_v = wb_kvb.t.rearrange("(k p) (h two d) -> p k h two d", p=128, two=2, d=128)
        for k in range(4):
            S.dma("sp", wkn[:, k, :, :], V(wb_kvb, kvb_v[:, k, :, 0, :]))
            S.dma("sp", wv[:, k, :, :], V(wb_kvb, kvb_v[:, k, :, 1, :]))
            qb_v = wb_qb.t.rearrange("(k p) (h d) -> p k h d", p=128, d=192)
            S.dma("sp", wqn[:, k, :, :], V(wb_qb, qb_v[:, k, :, 0:128]))
            S.dma("sp", wqr[:, k, :, :], V(wb_qb, qb_v[:, k, :, 128:192]))

        xt = [P.sb([128, D], F32) for _ in range(2)]
        xnb = [P.sb([128, D], BF16) for _ in range(2)]
        junk = P.sb([128, D], BF16)
        st4 = [P.sb([128, 4], F32) for _ in range(2)]
        xnT = [P.sb([128, 16, 512], BF16) for _ in range(1)]
        wt = [P.sb([128, 16, 256], BF16) for _ in range(3)]
        tpb = [P.ps([128, 8, 128], BF16) for _ in range(2)]
        pacc = [P.ps([128, 512], F32) for _ in range(4)]
        pst = [P.ps([128, 512], F32) for _ in range(2)]
        lr = [P.sb([128, 4, 512], F32) for _ in range(2)]
        lrn = [P.sb([128, 4, 512], BF16) for _ in range(2)]
        sqb = [P.sb([128, 4, 512], BF16) for _ in range(2)]
        rbc = [P.sb([128, 512], F32) for _ in range(3)]
        krope = P.sb([64, 512], F32)
        krw = P.sb([64, 512], F32)
        kR = P.sb([64, 512], F32)
        cosT = P.sb([64, 512], F32)
        sinT = P.sb([64, 512], F32)
        hf = [P.sb([128, 512], F32) for _ in range(2)]
        hr = [P.sb([64, 512], F32) for _ in range(2)]
        hsq = [P.sb([128, 512], BF16) for _ in range(2)]
        hsqr = [P.sb([64, 512], BF16) for _ in range(2)]
        ob = [P.sb([128, 512], BF16) for _ in range(4)]
        obr = [P.sb([64, 512], BF16) for _ in range(2)]
        gbt = [P.sb([128, 32], F32) for _ in range(2)]
        gbo = [P.sb([128, 32], F32) for _ in range(2)]
        cnt = {"w": 0, "p": 0, "o": 0, "r": 0, "h": 0, "g": 0, "t": 0}

        def nxt(lst, key):
            cnt[key] += 1
            return lst[cnt[key] % len(lst)]

        def rstd_from_ps(ps_v, dst_v, inv_n):
            S.ts("dve", dst_v, ps_v, inv_n, ALU.mult, EPS, ALU.add)
            S.act(dst_v, dst_v, AF.Sqrt)
            S.recip(dst_v, dst_v)

        tiles = [("c", 0, TC)] + [("l", 512 * i, 512) for i in range(T // 512)]
        for ti, (kind, t0, tw) in enumerate(tiles):
            u0 = t0 if kind == "c" else TC + t0
            src_d = ctx_d if kind == "c" else x_d
            srow, shrow = s1row, sh1row
            if ti == 1:
                S.dma("sp", s1row[:, :], V(rows_s, rows_s.t[0, :].partition_broadcast(128)))
                S.dma("sp", sh1row[:, :], V(rows_s, rows_s.t[1, :].partition_broadcast(128)))
            xT = xnT[0]
            for blk in range(tw // 128):
                xb_ = nxt(xt, "t")
                xn_ = xnb[cnt["t"] % 2]
                s4 = st4[cnt["t"] % 2]
                S.dma("sp", xb_[:, :], src_d[t0 + blk * 128:t0 + (blk + 1) * 128, :])
                S.act(junk[:, :], xb_[:, :], AF.Square, accum=s4[:, 0:1])
                S.ts("dve", s4[:, 1:2], s4[:, 0:1], 1.0 / D, ALU.mult, EPS, ALU.add)
                S.act(s4[:, 2:3], s4[:, 1:2], AF.Sqrt)
                S.recip(s4[:, 3:4], s4[:, 2:3])
                S.stt(xb_[:, :], xb_[:, :], s4[:, 3:4], srow[:, :], ALU.mult, ALU.mult)
                S.tt("dve", xn_[:, :], xb_[:, :], shrow[:, :], ALU.add)
                for half in range(2):
                    tp = tpb[half]
                    for k in range(8):
                        kc = half * 8 + k
                        S.tr(tp[:, k, :], xn_[:, kc * 128:(kc + 1) * 128], identb[:, :])
                    S.cp("act", xT[:, half * 8:half * 8 + 8, blk * 128:(blk + 1) * 128], tp[:, :, :])

            def proj_fm(c0, ncols, evac):
                for cb0 in range(0, ncols, 256):
                    ncb = min(256, ncols - cb0)
                    w = nxt(wt, "w")
                    S.dma("sp", w[:, :, 0:ncb], V(wb_in, wb_in.t[:, c0 + cb0:c0 + cb0 + ncb].rearrange("(k p) n -> p k n", p=128)))
                    for o in range(0, ncb, 128):
                        m = min(128, ncb - o)
                        pp = nxt(pacc, "p")
                        for k in range(16):
                            S.mm(pp[0:m, 0:tw], w[:, k, o:o + m], xT[:, k, 0:tw], start=(k == 0), stop=(k == 15))
                        evac((cb0 + o) // 128, pp, m)

            qa = lr[0]
            kva = lr[1]

            if kind == "l":
                proj_fm(0, 512, lambda oc, pp, m: S.cp("act", qa[:, oc, 0:tw], pp[:, 0:tw]))
            proj_fm(512, 512, lambda oc, pp, m: S.cp("act", kva[:, oc, 0:tw], pp[:, 0:tw]))
            proj_fm(1024, 64, lambda oc, pp, m: S.cp("act", krope[:, 0:tw], pp[0:64, 0:tw]))

            pre = prec if kind == "c" else prel

            def ev_pre(oc, pp, m):
                o_ = nxt(ob, "o")
                S.cp("act", o_[:, 0:tw], pp[:, 0:tw])
                S.dma("pool", pre[oc * 128:(oc + 1) * 128, 1 + t0:1 + t0 + tw], o_[:, 0:tw])
            proj_fm(1088, 3072, ev_pre)

            for blk in range(tw // 128):
                tsl = slice(blk * 128, (blk + 1) * 128)
                w = nxt(wt, "w")
                S.dma("sp", w[:, :, 0:32], V(wb_in, wb_in.t[:, 5184:5216].rearrange("(k p) n -> p k n", p=128)))
                pp = nxt(pacc, "p")
                for k in range(16):
                    S.mm(pp[:, 0:32], xT[:, k, tsl], w[:, k, 0:32], start=(k == 0), stop=(k == 15))
                g_ = nxt(gbt, "g")
                go = gbo[cnt["g"] % 2]
                S.tt("dve", g_[:, 0:16], pp[:, 0:16], dtb[:, :], ALU.add)
                S.act(g_[:, 0:16], g_[:, 0:16], AF.Exp)
                S.act(g_[:, 0:16], g_[:, 0:16], AF.Ln, bias=1.0)
                S.tt("dve", go[:, 0:16], g_[:, 0:16], nexpa[:, :], ALU.mult)
                S.act(go[:, 16:32], pp[:, 16:32], AF.Sigmoid)
                S.dma("pool", gb_s[u0 + blk * 128:u0 + (blk + 1) * 128, :], go[:, :])
                if kind == "l":
                    for qq in range(4):
                        w = nxt(wt, "w")
                        S.dma("sp", w[:, :, 0:256], V(wb_in, wb_in.t[:, 4160 + qq * 256:4160 + (qq + 1) * 256].rearrange("(k p) n -> p k n", p=128)))
                        pp = nxt(pacc, "p")
                        for k in range(16):
                            S.mm(pp[:, 0:256], xT[:, k, tsl], w[:, k, 0:256], start=(k == 0), stop=(k == 15))
                        o_ = nxt(ob, "o")
                        S.act(o_[:, 0:256], pp[:, 0:256], AF.Silu)
                        S.dma("pool", zs_s[t0 + blk * 128:t0 + (blk + 1) * 128, qq * 256:(qq + 1) * 256], o_[:, 0:256])
            if kind == "l":
                def ev_gate(oc, pp, m):
                    o_ = nxt(ob, "o")
                    S.act(o_[:, 0:tw], pp[:, 0:tw], AF.Sigmoid)
                    S.dma("pool", ga_s[oc * 128:(oc + 1) * 128, t0:t0 + tw], o_[:, 0:tw])
                proj_fm(5216, 4096, ev_gate)

            def lr_norm(raw, nw, dst, sq):
                S.act(sq[:, :, 0:tw], raw[:, :, 0:tw], AF.Square)
                pp = nxt(pst, "r")
                for c in range(4):
                    S.mm(pp[:, 0:tw], onesb[:, :], sq[:, c, 0:tw], start=(c == 0), stop=(c == 3))
                r_ = nxt(rbc, "h")
                rstd_from_ps(pp[:, 0:tw], r_[:, 0:tw], 1.0 / 512)
                for c in range(4):
                    S.stt(dst[:, c, 0:tw], raw[:, c, 0:tw], nw[:, c:c + 1], r_[:, 0:tw], ALU.mult, ALU.mult)

            S.dma("sp", cosT[:, 0:tw], cos_d[:, u0:u0 + tw])
            S.dma("sp", sinT[:, 0:tw], sin_d[:, u0:u0 + tw])

            def rope(src, wcol, dst):
                S.ts("dve", krw[:, 0:tw], src, wcol, ALU.mult)
                pp = nxt(pst, "r")
                S.mm(pp[0:64, 0:tw], protT[:, :], krw[:, 0:tw])
                S.tt("dve", dst, pp[0:64, 0:tw], sinT[:, 0:tw], ALU.mult)
                S.tt("dve", krw[:, 0:tw], krw[:, 0:tw], cosT[:, 0:tw], ALU.mult)
                S.tt("dve", dst, dst, krw[:, 0:tw], ALU.add)

            kvn = lrn[1]
            lr_norm(kva, kvanw, kvn, sqb[1])
            rope(krope[:, 0:tw], knw[0:64, 1:2], kR[:, 0:tw])
            ksq = hsqr[0]
            S.act(ksq[:, 0:tw], krope[:, 0:tw], AF.Square)
            for h in range(H):
                pp = nxt(pacc, "p")
                for c in range(4):
                    S.mm(pp[:, 0:tw], wkn[:, c, h, :], kvn[:, c, 0:tw], start=(c == 0), stop=(c == 3))
                f_ = nxt(hf, "h")
                q_ = hsq[cnt["h"] % 2]
                S.cp("act", f_[:, 0:tw], pp[:, 0:tw])
                S.act(q_[:, 0:tw], pp[:, 0:tw], AF.Square)
                p2 = nxt(pst, "r")
                S.mm(p2[:, 0:tw], onesb[:, :], q_[:, 0:tw], start=True, stop=False)
                S.mm(p2[:, 0:tw], onesb[0:64, :], ksq[:, 0:tw], start=False, stop=True)
                r_ = nxt(rbc, "h")
                rstd_from_ps(p2[:, 0:tw], r_[:, 0:tw], 1.0 / 192)
                o_ = nxt(ob, "o")
                S.stt(o_[:, 0:tw], f_[:, 0:tw], knw[:, 0:1], r_[:, 0:tw], ALU.mult, ALU.mult)
                S.dma("pool", kt_s[h, 0:128, u0:u0 + tw], o_[:, 0:tw])
                o2 = nxt(obr, "o")
                S.tt("dve", o2[:, 0:tw], kR[:, 0:tw], r_[0:64, 0:tw], ALU.mult)
                S.dma("pool", kt_s[h, 128:192, u0:u0 + tw], o2[:, 0:tw])
            for blk in range(tw // 128):
                tsl = slice(blk * 128, (blk + 1) * 128)
                for g2_ in range(2):
                    pp = nxt(pacc, "p")
                    for c in range(4):
                        S.mm(pp[:, :], kvn[:, c, tsl], wv[:, c, g2_ * 4:(g2_ + 1) * 4, :].rearrange("p h d -> p (h d)"), start=(c == 0), stop=(c == 3))
                    o_ = nxt(ob, "o")
                    S.cp("act", o_[:, :], pp[:, :])
                    S.dma("pool", vs_s[u0 + blk * 128:u0 + (blk + 1) * 128, g2_ * 512:(g2_ + 1) * 512], o_[:, :])
            if kind == "l":
                qn_ = lrn[0]
                lr_norm(qa, qanw, qn_, sqb[0])
                for h in range(H):
                    pp = nxt(pacc, "p")
                    for c in range(4):
                        S.mm(pp[:, 0:tw], wqn[:, c, h, :], qn_[:, c, 0:tw], start=(c == 0), stop=(c == 3))
                    ppr = nxt(pacc, "p")
                    for c in range(4):
                        S.mm(ppr[0:64, 0:tw], wqr[:, c, h, :], qn_[:, c, 0:tw], start=(c == 0), stop=(c == 3))
                    f_ = nxt(hf, "h")
                    q_ = hsq[cnt["h"] % 2]
                    fr = hr[cnt["h"] % 2]
                    qr_ = hsqr[1]
                    S.cp("act", f_[:, 0:tw], pp[:, 0:tw])
                    S.act(q_[:, 0:tw], pp[:, 0:tw], AF.Square)
                    S.cp("act", fr[:, 0:tw], ppr[0:64, 0:tw])
                    S.act(qr_[:, 0:tw], ppr[0:64, 0:tw], AF.Square)
                    p2 = nxt(pst, "r")
                    S.mm(p2[:, 0:tw], onesb[:, :], q_[:, 0:tw], start=True, stop=False)
                    S.mm(p2[:, 0:tw], onesb[0:64, :], qr_[:, 0:tw], start=False, stop=True)
                    r_ = nxt(rbc, "h")
                    rstd_from_ps(p2[:, 0:tw], r_[:, 0:tw], 1.0 / 192)
                    S.ts("dve", r_[:, 0:tw], r_[:, 0:tw], QSC, ALU.mult)
                    o_ = nxt(ob, "o")
                    S.stt(o_[:, 0:tw], f_[:, 0:tw], qnw[:, 0:1], r_[:, 0:tw], ALU.mult, ALU.mult)
                    S.dma("pool", qt_s[h, 0:128, t0:t0 + tw], o_[:, 0:tw])
                    rope(fr[:, 0:tw], qnw[0:64, 1:2], kR[:, 0:tw])
                    o2 = nxt(obr, "o")
                    S.tt("dve", o2[:, 0:tw], kR[:, 0:tw], r_[0:64, 0:tw], ALU.mult)
                    S.dma("pool", qt_s[h, 128:192, t0:t0 + tw], o2[:, 0:tw])
        P.end()

        if stop_after == "B":
            return nc, S
        P = Phase()
        identb = P.sb([128, 128], BF16)
        identf = P.sb([128, 128], F32)
        S.dma("sp", identf[:, :], identf_d[:, :])
        S.cp("dve", identb[:, :], identf[:, :])
        onesb = P.sb([128, 128], BF16)
        S.memset("dve", onesb[:, :], 1.0)
        cw = P.sb([128, 24, 3], F32)
        S.dma("sp", cw[:, :, :], dn_conv_d[:, :, :])
        win = [P.sb([128, 514], BF16) for _ in range(3)]
        c1 = [P.sb([128, 512], F32) for _ in range(2)]
        a1 = [P.sb([128, 512], F32) for _ in range(2)]
        sq1 = [P.sb([128, 512], BF16) for _ in range(2)]
        r1 = [P.sb([128, 512], F32) for _ in range(2)]
        kTb = [P.sb([128, 512], BF16) for _ in range(3)]
        tko = [P.sb([128, 4, 128], BF16) for _ in range(3)]
        pss = [P.ps([128, 512], F32) for _ in range(2)]
        ptp = [P.ps([128, 4, 128], BF16) for _ in range(2)]
        n_ = 0
        for (kind, t0, tw) in tiles:
            u0 = t0 if kind == "c" else TC + t0
            pre = prec if kind == "c" else prel
            for role in ((1, 2) if kind == "c" else (0, 1, 2)):
                for h in range(H):
                    n_ += 1
                    ci = role * 8 + h
                    w_ = win[n_ % 3]
                    S.dma("sp", w_[:, 0:tw + 2], pre[ci * 128:(ci + 1) * 128, t0:t0 + tw + 2])
                    c_ = c1[n_ % 2]
                    S.ts("dve", c_[:, 0:tw], w_[:, 0:tw], cw[:, ci, 0:1], ALU.mult)
                    S.stt(c_[:, 0:tw], w_[:, 1:tw + 1], cw[:, ci, 1:2], c_[:, 0:tw], ALU.mult, ALU.add)
                    S.stt(c_[:, 0:tw], w_[:, 2:tw + 2], cw[:, ci, 2:3], c_[:, 0:tw], ALU.mult, ALU.add)
                    kb_ = kTb[n_ % 3]
                    if role == 2:
                        S.act(kb_[:, 0:tw], c_[:, 0:tw], AF.Silu)
                    else:
                        a_ = a1[n_ % 2]
                        S.act(a_[:, 0:tw], c_[:, 0:tw], AF.Silu)
                        s_ = sq1[n_ % 2]
                        S.act(s_[:, 0:tw], a_[:, 0:tw], AF.Square)
                        pp = pss[n_ % 2]
                        S.mm(pp[:, 0:tw], onesb[:, :], s_[:, 0:tw])
                        r_ = r1[n_ % 2]
                        S.ts("dve", r_[:, 0:tw], pp[:, 0:tw], EPS, ALU.add)
                        S.act(r_[:, 0:tw], r_[:, 0:tw], AF.Sqrt)
                        S.recip(r_[:, 0:tw], r_[:, 0:tw])
                        if role == 0:
                            S.stt(kb_[:, 0:tw], a_[:, 0:tw], 128.0 ** -0.5, r_[:, 0:tw], ALU.mult, ALU.mult)
                            S.dma("pool", dqt_s[h * 128:(h + 1) * 128, t0:t0 + tw], kb_[:, 0:tw])
                        else:
                            S.tt("dve", kb_[:, 0:tw], a_[:, 0:tw], r_[:, 0:tw], ALU.mult)
                            S.dma("pool", dkt_s[h * 128:(h + 1) * 128, u0:u0 + tw], kb_[:, 0:tw])
                    if role >= 1:
                        nb = tw // 128
                        tp = ptp[n_ % 2]
                        for blk in range(nb):
                            S.tr(tp[:, blk, :], kb_[:, blk * 128:(blk + 1) * 128], identb[:, :])
                        to = tko[n_ % 3]
                        S.cp("act", to[:, 0:nb, :], tp[:, 0:nb, :])
                        dst = dktok_s if role == 1 else dvtok_s
                        S.dma("pool", V(dst, dst.t[u0:u0 + tw, h * 128:(h + 1) * 128].rearrange("(b p) d -> p b d", p=128)), to[:, 0:nb, :])
        P.end()
        if stop_after == "C":
            return nc, S

        P = Phase()
        identb = P.sb([128, 128], BF16)
        identf = P.sb([128, 128], F32)
        S.dma("sp", identf[:, :], identf_d[:, :])
        S.cp("dve", identb[:, :], identf[:, :])
        ones64 = P.sb([64, 128], F32)
        S.memset("dve", ones64[:, :], 1.0)
        dm = P.sb([64, 2, 4, 64], F32)
        S.dma("sp", dm[:, :, :, :], dnm_d[:, :, :, :])
        onr = P.sb([64, 128], F32)
        S.dma("sp", onr[:, :], V(onorm_d, onorm_d.t.partition_broadcast(64)))
        S32 = [P.sb([128, 128], F32) for _ in range(H)]
        S16 = [P.sb([128, 128], BF16) for _ in range(H)]
        NS = 2
        kT_ = [P.sb([128, 8, 64], BF16) for _ in range(NS)]
        qT_ = [P.sb([128, 8, 64], BF16) for _ in range(NS)]
        ktk = [P.sb([64, 8, 128], BF16) for _ in range(NS)]
        vtk = [P.sb([64, 8, 128], BF16) for _ in range(NS)]
        gbc = [P.sb([64, 32], F32) for _ in range(NS)]
        ofl = [P.sb([64, 8, 128], F32) for _ in range(NS)]
        zsl = [P.sb([64, 8, 128], BF16) for _ in range(NS)]
        gcs = P.sb([64, 16], F32)
        eg = P.sb([64, 8], F32)
        ek = P.sb([64, 8], F32)
        beg = P.sb([64, 8], F32)
        egend = P.sb([128, 8], F32)
        Gm = P.sb([64, 8, 64], F32)
        Dc = P.sb([64, 8, 64], F32)
        t1 = P.sb([64, 8, 64], F32)
        t2 = P.sb([64, 8, 64], F32)
        BM = P.sb([64, 8, 64], F32)
        Mm = P.sb([64, 8, 64], F32)
        Nn = P.sb([64, 8, 64], F32)
        Aq = P.sb([64, 8, 64], BF16)
        AqT = P.sb([64, 8, 64], BF16)
        PMb = [P.sb([64, 8, 64], F32) for _ in range(2)]
        PNb = [P.sb([64, 8, 64], F32) for _ in range(2)]
        TNb = [P.sb([64, 8, 64], F32) for _ in range(2)]
        TbT = P.sb([64, 8, 64], BF16)
        TbgT = P.sb([64, 8, 64], BF16)
        kend = P.sb([64, 8, 128], BF16)
        nwT = [P.sb([128, 64], BF16) for _ in range(2)]
        vnew = [P.sb([64, 128], BF16) for _ in range(2)]
        o2s = [P.sb([64, 128], F32) for _ in range(2)]
        osb = [P.sb([64, 8, 128], F32) for _ in range(2)]
        sqo = P.sb([64, 8, 128], F32)
        ms = P.sb([64, 8], F32)
        yb = P.sb([64, 8, 128], BF16)
        ydT = [P.sb([128, 8, 64], BF16) for _ in range(2)]
        pgc = P.ps([128, 512], F32)
        pG = P.ps([128, 512], F32)
        pA = P.ps([128, 512], F32)
        pQ = P.ps([128, 512], F32)
        pN = P.ps([128, 512], F32)
        pT16 = P.ps([128, 8, 128], BF16)
        pH = P.ps([128, 512], F32)
        pS = P.ps([128, 512], F32)
        for h in range(H):
            S.memset("dve", S32[h][:, :], 0.0)
            S.memset("dve", S16[h][:, :], 0.0)
        i64f = identf[0:64, 0:64]
        i64b = identb[0:64, 0:64]
        sh = [64, 8, 64]

        def dn_chunk(n, d, u0, t0, with_out, combine):
            s_ = n % NS
            kT, qT, ktok, vtok, gb = kT_[s_], qT_[s_], ktk[s_], vtk[s_], gbc[s_]
            S.dma("sp", kT[:, :, :], V(dkt_s, dkt_s.t[:, u0:u0 + 64].rearrange("(h d) t -> d h t", d=128)))
            S.dma("sp", ktok[:, :, :], V(dktok_s, dktok_s.t[u0:u0 + 64, :].rearrange("t (h d) -> t h d", d=128)))
            S.dma("sp", vtok[:, :, :], V(dvtok_s, dvtok_s.t[u0:u0 + 64, :].rearrange("t (h d) -> t h d", d=128)))
            S.dma("sp", gb[:, :], gb_s[u0:u0 + 64, :])
            if with_out:
                S.dma("sp", qT[:, :, :], V(dqt_s, dqt_s.t[:, t0:t0 + 64].rearrange("(h d) t -> d h t", d=128)))
            if combine:
                S.dma("sp", ofl[s_][:, :, :], V(of_s, of_s.t[t0:t0 + 64, :].rearrange("t (h d) -> t h d", d=128)))
                S.dma("sp", zsl[s_][:, :, :], V(zs_s, zs_s.t[t0:t0 + 64, :].rearrange("t (h d) -> t h d", d=128)))
            Cm, Xm, Xs, Xi = (dm[:, d, j, :] for j in range(4))
            g = gb[:, d * 8:(d + 1) * 8]
            beta = gb[:, 16 + d * 8:16 + (d + 1) * 8]
            S.mm(pgc[0:64, 0:8], Cm, g)
            S.mm(pgc[:, 8:16], ones64[:, :], g)
            S.cp("dve", gcs[:, :], pgc[0:64, 0:16])
            S.act(eg[:, :], gcs[:, 0:8], AF.Exp)
            S.act(egend[:, :], pgc[:, 8:16], AF.Exp)
            S.tt("dve", ek[:, :], gcs[:, 8:16], gcs[:, 0:8], ALU.subtract)
            S.act(ek[:, :], ek[:, :], AF.Exp)
            S.tt("dve", beg[:, :], beta, eg[:, :], ALU.mult)
            S.tt("dve", Gm[:, :, :], Xm.unsqueeze(1).bc(sh), g.unsqueeze(2).bc(sh), ALU.mult)
            S.mm(pG[0:64, :], Cm, Gm[:, :, :].rearrange("p h j -> p (h j)"))
            S.act(Dc[:, :, :].rearrange("p h j -> p (h j)"), pG[0:64, :], AF.Exp)
            for h in range(H):
                S.mm(pA[0:64, h * 64:(h + 1) * 64], kT[:, h, :], kT[:, h, :])
            S.tt("dve", t1[:, :, :].rearrange("p h j -> p (h j)"), pA[0:64, :], Dc[:, :, :].rearrange("p h j -> p (h j)"), ALU.mult)
            S.tt("pool", BM[:, :, :], Xs.unsqueeze(1).bc(sh), beta.unsqueeze(2).bc(sh), ALU.mult)
            S.tt("pool", Mm[:, :, :], t1[:, :, :], BM[:, :, :], ALU.mult)
            if with_out:
                for h in range(H):
                    S.mm(pQ[0:64, h * 64:(h + 1) * 64], qT[:, h, :], kT[:, h, :])
                S.tt("dve", t2[:, :, :].rearrange("p h j -> p (h j)"), pQ[0:64, :], Dc[:, :, :].rearrange("p h j -> p (h j)"), ALU.mult)
                S.tt("pool", Aq[:, :, :], t2[:, :, :], Xi.unsqueeze(1).bc(sh), ALU.mult)
                for h in range(H):
                    S.tr(pT16[0:64, h, 0:64], Aq[:, h, :], i64b)
                S.cp("act", AqT[:, :, :], pT16[0:64, :, 0:64])
            for h in range(H):
                S.tr(pN[0:64, h * 64:(h + 1) * 64], Mm[:, h, :], i64f)
            S.cp("act", Nn[:, :, :].rearrange("p h j -> p (h j)"), pN[0:64, :])
            TN = TNb[0]
            S.tt("dve", TN[:, :, :], i64f.unsqueeze(1).bc(sh), Nn[:, :, :], ALU.subtract)
            PM, PN = Mm, Nn
            for lv in range(1, 6):
                PM2, PN2, TN2 = PMb[lv % 2], PNb[lv % 2], TNb[lv % 2]
                if lv < 5:
                    for h in range(H):
                        S.mm(pG[0:64, h * 64:(h + 1) * 64], PM[:, h, :], PN[:, h, :])
                for h in range(H):
                    S.mm(pA[0:64, h * 64:(h + 1) * 64], PN[:, h, :], PM[:, h, :])
                if lv < 5:
                    S.cp("act", PN2[:, :, :].rearrange("p h j -> p (h j)"), pG[0:64, :])
                S.cp("dve", PM2[:, :, :].rearrange("p h j -> p (h j)"), pA[0:64, :])
                for h in range(H):
                    S.mm(pQ[0:64, h * 64:(h + 1) * 64], PM2[:, h, :], TN[:, h, :])
                S.tt("dve", TN2[:, :, :].rearrange("p h j -> p (h j)"), pQ[0:64, :], TN[:, :, :].rearrange("p h j -> p (h j)"), ALU.add)
                PM, PN, TN = PM2, PN2, TN2
            S.tt("pool", TbT[:, :, :], TN[:, :, :], beta.unsqueeze(2).bc(sh), ALU.mult)
            S.tt("pool", TbgT[:, :, :], TN[:, :, :], beg[:, :].unsqueeze(2).bc(sh), ALU.mult)
            S.tt("pool", kend[:, :, :], ktok[:, :, :], ek[:, :].unsqueeze(2).bc([64, 8, 128]), ALU.mult)
            o_ = osb[n % 2]
            for h in range(H):
                nw = nwT[h % 2]
                vn = vnew[h % 2]
                S.mm(pH[:, 0:64], ktok[:, h, :], TbgT[:, h, :])
                S.act(nw[:, :], pH[:, 0:64], AF.Copy, scale=-1.0)
                S.mm(pH[0:64, 64:192], TbT[:, h, :], vtok[:, h, :], start=True, stop=False)
                S.mm(pH[0:64, 64:192], nw[:, :], S16[h][:, :], start=False, stop=True)
                S.cp("act", vn[:, :], pH[0:64, 64:192])
                if with_out:
                    S.mm(pH[0:64, 192:320], qT[:, h, :], S16[h][:, :])
                    S.mm(pH[0:64, 320:448], AqT[:, h, :], vn[:, :])
                    S.cp("act", o2s[h % 2][:, :], pH[0:64, 320:448])
                    S.stt(o_[:, h, :], pH[0:64, 192:320], eg[:, h:h + 1], o2s[h % 2][:, :], ALU.mult, ALU.add)
                S.mm(pS[:, (h % 4) * 128:(h % 4 + 1) * 128], kend[:, h, :], vn[:, :])
                S.stt(S32[h][:, :], S32[h][:, :], egend[:, h:h + 1], pS[:, (h % 4) * 128:(h % 4 + 1) * 128], ALU.mult, ALU.add)
                S.cp("pool", S16[h][:, :], S32[h][:, :])
            if with_out and not combine:
                S.dma("pool", V(of_s, of_s.t[t0:t0 + 64, :].rearrange("t (h d) -> t h d", d=128)), o_[:, :, :])
            if combine:
                S.tt("pool", o_[:, :, :], o_[:, :, :], ofl[s_][:, :, :], ALU.add)
                S.tt("pool", sqo[:, :, :], o_[:, :, :], o_[:, :, :], ALU.mult)
                S.reduce(ms[:, :], sqo[:, :, :])
                S.ts("dve", ms[:, :], ms[:, :], 1.0 / 128, ALU.mult, EPS, ALU.add)
                S.act(ms[:, :], ms[:, :], AF.Sqrt)
                S.recip(ms[:, :], ms[:, :])
                S.tt("pool", sqo[:, :, :], o_[:, :, :], ms[:, :].unsqueeze(2).bc([64, 8, 128]), ALU.mult)
                S.tt("pool", sqo[:, :, :], sqo[:, :, :], onr[:, :].unsqueeze(1).bc([64, 8, 128]), ALU.mult)
                S.tt("pool", yb[:, :, :], sqo[:, :, :], zsl[s_][:, :, :], ALU.mult)
                for h in range(H):
                    S.tr(pT16[:, h, 0:64], yb[:, h, :], i64b)
                yo = ydT[n % 2]
                S.cp("act", yo[:, :, :], pT16[:, :, 0:64])
                S.dma("pool", V(yd_s, yd_s.t[:, t0:t0 + 64].rearrange("(h d) t -> d h t", d=128)), yo[:, :, :])

        n = 0
        NCK = T // 64
        for d in (0, 1):
            for h in range(H):
                S.memset("dve", S32[h][:, :], 0.0)
                S.memset("dve", S16[h][:, :], 0.0)
            cl = range(TC // 64) if d == 0 else range(TC // 64 - 1, -1, -1)
            for c in cl:
                dn_chunk(n, d, c * 64, 0, False, False)
                n += 1
            ll = range(NCK) if d == 0 else range(NCK - 1, -1, -1)
            for c in ll:
                dn_chunk(n, d, TC + c * 64, c * 64, True, d == 1)
                n += 1
            S.flush()
        P.end()
        if stop_after == "D":
            return nc, S

        P = Phase()
        onesb = P.sb([128, 128], BF16)
        S.memset("dve", onesb[:, :], 1.0)
        Kn = P.sb([128, NTOK], BF16)
        Kr = P.sb([64, NTOK], BF16)
        Vh = P.sb([128, NTOK // 128, 128], BF16)
        Qn = [P.sb([128, 512], BF16) for _ in range(2)]
        Qr = [P.sb([64, 512], BF16) for _ in range(2)]
        Pt = [P.sb([128, 512], BF16) for _ in range(3)]
        rd = P.sb([128, 512], F32)
        yo_ = [P.sb([128, 512], BF16) for _ in range(2)]
        sps = [P.ps([128, 512], F32) for _ in range(3)]
        ops_ = [P.ps([128, 512], F32) for _ in range(2)]
        dps = [P.ps([128, 512], F32) for _ in range(2)]
        NKB = NTOK // 128
        qi = 0
        for h in range(H):
            KST = NTOK // 4
            for c0 in range(0, NTOK, KST):
                S.dma("sp", Kn[:, c0:c0 + KST], kt_s[h, 0:128, c0:c0 + KST])
                S.dma("sp", Kr[:, c0:c0 + KST], kt_s[h, 128:192, c0:c0 + KST])
            VST = max(1, NKB // 5)
            for b0 in range(0, NKB, VST):
                b1 = min(b0 + VST, NKB)
                S.dma("sp", Vh[:, b0:b1, :], V(vs_s, vs_s.t[b0 * 128:b1 * 128, h * 128:(h + 1) * 128].rearrange("(b p) d -> p b d", p=128)))
            for t0 in range(0, T, 512):
                qi += 1
                qn_, qr_ = Qn[qi % 2], Qr[qi % 2]
                S.dma("sp", qn_[:, :], qt_s[h, 0:128, t0:t0 + 512])
                S.dma("sp", qr_[:, :], qt_s[h, 128:192, t0:t0 + 512])
                po, pd = ops_[qi % 2], dps[qi % 2]
                for kb in range(NKB):
                    sp_ = sps[kb % 3]
                    pt = Pt[kb % 3]
                    S.mm(sp_[:, :], Kn[:, kb * 128:(kb + 1) * 128], qn_[:, :], start=True, stop=False)
                    S.mm(sp_[:, :], Kr[:, kb * 128:(kb + 1) * 128], qr_[:, :], start=False, stop=True)
                    S.act(pt[:, :], sp_[:, :], AF.Exp)
                    S.mm(po[:, :], Vh[:, kb, :], pt[:, :], start=(kb == 0), stop=(kb == NKB - 1))
                    S.mm(pd[:, :], onesb[:, :], pt[:, :], start=(kb == 0), stop=(kb == NKB - 1))
                S.recip(rd[:, :], pd[:, :])
                y_ = yo_[qi % 2]
                S.tt("dve", y_[:, :], po[:, :], rd[:, :], ALU.mult)
                S.dma("pool", ym_s[h * 128:(h + 1) * 128, t0:t0 + 512], y_[:, :])
        P.end()
        if stop_after == "E":
            return nc, S

        P = Phase()
        g1row = load_row(P, 4)
        ymt = P.sb([128, 8, 512], BF16)
        ydt = P.sb([128, 8, 512], BF16)
        gat = [P.sb([128, 512], BF16) for _ in range(4)]
        wo1 = [P.sb([128, 8, 128], BF16) for _ in range(2)]
        wo2 = [P.sb([128, 8, 128], BF16) for _ in range(2)]
        mg = P.sb([128, 16, 512], BF16)
        tm1 = [P.sb([128, 512], F32) for _ in range(2)]
        tm2 = [P.sb([128, 512], F32) for _ in range(2)]
        wout = [P.sb([128, 16, 512], BF16) for _ in range(2)]
        xr = [P.sb([128, 1024], F32) for _ in range(2)]
        pm1 = [P.ps([128, 512], F32) for _ in range(2)]
        pm2 = [P.ps([128, 512], F32) for _ in range(2)]
        py = [P.ps([128, 512], F32) for _ in range(3)]
        n_ = 0
        for t0 in range(0, T, 512):
            S.dma("sp", ymt[:, :, :], V(ym_s, ym_s.t[:, t0:t0 + 512].rearrange("(k p) t -> p k t", p=128)))
            S.dma("sp", ydt[:, :, :], V(yd_s, yd_s.t[:, t0:t0 + 512].rearrange("(k p) t -> p k t", p=128)))
            for oc in range(16):
                n_ += 1
                w1, w2 = wo1[n_ % 2], wo2[n_ % 2]
                S.dma("sp", w1[:, :, :], V(wb_omla, wb_omla.t[:, oc * 128:(oc + 1) * 128].rearrange("(k p) n -> p k n", p=128)))
                S.dma("sp", w2[:, :, :], V(wb_odn, wb_odn.t[:, oc * 128:(oc + 1) * 128].rearrange("(k p) n -> p k n", p=128)))
                ga1, ga2 = gat[(2 * n_) % 4], gat[(2 * n_ + 1) % 4]
                S.dma("sp", ga1[:, :], ga_s[oc * 128:(oc + 1) * 128, t0:t0 + 512])
                S.dma("sp", ga2[:, :], ga_s[D + oc * 128:D + (oc + 1) * 128, t0:t0 + 512])
                p1, p2 = pm1[n_ % 2], pm2[n_ % 2]
                for k in range(8):
                    S.mm(p1[:, :], w1[:, k, :], ymt[:, k, :], start=(k == 0), stop=(k == 7))
                for k in range(8):
                    S.mm(p2[:, :], w2[:, k, :], ydt[:, k, :], start=(k == 0), stop=(k == 7))
                a_, b_ = tm1[n_ % 2], tm2[n_ % 2]
                S.tt("dve", a_[:, :], p1[:, :], ga1[:, :], ALU.mult)
                S.tt("dve", b_[:, :], p2[:, :], ga2[:, :], ALU.mult)
                S.tt("pool", mg[:, oc, :], a_[:, :], b_[:, :], ALU.add)
            for cb in range(4):
                wo = wout[cb % 2]
                S.dma("sp", wo[:, :, :], V(wb_out, wb_out.t[:, cb * 512:(cb + 1) * 512].rearrange("(k p) n -> p k n", p=128)))
                for blk in range(4):
                    n_ += 1
                    pp = py[n_ % 3]
                    for k in range(16):
                        S.mm(pp[:, :], mg[:, k, blk * 128:(blk + 1) * 128], wo[:, k, :], start=(k == 0), stop=(k == 15))
                    a_ = xr[n_ % 2]
                    S.dma("sp", a_[:, 512:1024], x_d[t0 + blk * 128:t0 + (blk + 1) * 128, cb * 512:(cb + 1) * 512])
                    S.tt("dve", a_[:, 0:512], pp[:, :], g1row[:, cb * 512:(cb + 1) * 512], ALU.mult)
                    S.tt("pool", a_[:, 0:512], a_[:, 0:512], a_[:, 512:1024], ALU.add)
                    S.dma("pool", x1_s[t0 + blk * 128:t0 + (blk + 1) * 128, cb * 512:(cb + 1) * 512], a_[:, 0:512])
        P.end()
        if stop_after == "F":
            return nc, S

        P = Phase()
        identb = P.sb([128, 128], BF16)
        identf = P.sb([128, 128], F32)
        S.dma("sp", identf[:, :], identf_d[:, :])
        S.cp("dve", identb[:, :], identf[:, :])
        s2row = load_row(P, 5)
        sh2row = load_row(P, 6)
        xt = [P.sb([128, D], F32) for _ in range(2)]
        xnb = [P.sb([128, D], BF16) for _ in range(2)]
        junk = P.sb([128, D], BF16)
        st4 = [P.sb([128, 4], F32) for _ in range(2)]
        xT = P.sb([128, 16, 512], BF16)
        wt = [P.sb([128, 16, 256], BF16) for _ in range(3)]
        ob = [P.sb([128, 512], BF16) for _ in range(4)]
        tpb = [P.ps([128, 8, 128], BF16) for _ in range(2)]
        pacc = [P.ps([128, 512], F32) for _ in range(4)]
        n_ = 0
        for t0 in range(0, T, 512):
            for blk in range(4):
                n_ += 1
                xb_, xn_, s4 = xt[n_ % 2], xnb[n_ % 2], st4[n_ % 2]
                S.dma("sp", xb_[:, :], x1_s[t0 + blk * 128:t0 + (blk + 1) * 128, :])
                S.act(junk[:, :], xb_[:, :], AF.Square, accum=s4[:, 0:1])
                S.ts("dve", s4[:, 1:2], s4[:, 0:1], 1.0 / D, ALU.mult, EPS, ALU.add)
                S.act(s4[:, 2:3], s4[:, 1:2], AF.Sqrt)
                S.recip(s4[:, 3:4], s4[:, 2:3])
                S.stt(xb_[:, :], xb_[:, :], s4[:, 3:4], s2row[:, :], ALU.mult, ALU.mult)
                S.tt("dve", xn_[:, :], xb_[:, :], sh2row[:, :], ALU.add)
                for half in range(2):
                    tp = tpb[half]
                    for k in range(8):
                        kc = half * 8 + k
                        S.tr(tp[:, k, :], xn_[:, kc * 128:(kc + 1) * 128], identb[:, :])
                    S.cp("act", xT[:, half * 8:half * 8 + 8, blk * 128:(blk + 1) * 128], tp[:, :, :])
            for cb0 in range(0, 2 * FH, 256):
                w = wt[(cb0 // 256) % 3]
                S.dma("sp", w[:, :, :], V(wb_up, wb_up.t[:, cb0:cb0 + 256].rearrange("(k p) n -> p k n", p=128)))
                for o in range(0, 256, 128):
                    n_ += 1
                    pp = pacc[n_ % 4]
                    for k in range(16):
                        S.mm(pp[:, :], w[:, k, o:o + 128], xT[:, k, :], start=(k == 0), stop=(k == 15))
                    o_ = ob[n_ % 4]
                    S.cp("act", o_[:, :], pp[:, :])
                    c = cb0 + o
                    if c < FH:
                        S.dma("pool", gp_s[c:c + 128, 1 + t0:1 + t0 + 512], o_[:, :])
                    else:
                        S.dma("pool", val_s[c - FH:c - FH + 128, t0:t0 + 512], o_[:, :])
        P.end()
        if stop_after == "G":
            return nc, S

        P = Phase()
        g2row = load_row(P, 7)
        fcw = P.sb([128, 43, 3], F32)
        S.dma("sp", fcw[:, :, :], fconv_d[:, :, :])
        hT = P.sb([128, 43, 512], BF16)
        win = [P.sb([128, 514], BF16) for _ in range(3)]
        vl = [P.sb([128, 512], BF16) for _ in range(3)]
        c1 = [P.sb([128, 512], F32) for _ in range(2)]
        a1 = [P.sb([128, 512], F32) for _ in range(2)]
        wd = [P.sb([128, 43, 512], BF16) for _ in range(2)]
        xo = [P.sb([128, 1024], F32) for _ in range(4)]
        pd_ = [P.ps([128, 512], F32) for _ in range(4)]
        n_ = 0
        for t0 in range(0, T, 512):
            for c in range(43):
                n_ += 1
                w_, v_, c_, a_ = win[n_ % 3], vl[n_ % 3], c1[n_ % 2], a1[n_ % 2]
                S.dma("sp", w_[:, :], gp_s[c * 128:(c + 1) * 128, t0:t0 + 514])
                S.dma("sp", v_[:, :], val_s[c * 128:(c + 1) * 128, t0:t0 + 512])
                S.ts("dve", c_[:, :], w_[:, 0:512], fcw[:, c, 0:1], ALU.mult)
                S.stt(c_[:, :], w_[:, 1:513], fcw[:, c, 1:2], c_[:, :], ALU.mult, ALU.add)
                S.stt(c_[:, :], w_[:, 2:514], fcw[:, c, 2:3], c_[:, :], ALU.mult, ALU.add)
                S.act(a_[:, :], c_[:, :], AF.Silu)
                S.tt("pool", hT[:, c, :], a_[:, :], v_[:, :], ALU.mult)
            for cb in range(4):
                w = wd[cb % 2]
                S.dma("sp", w[:, :, :], V(wb_dn, wb_dn.t[:, cb * 512:(cb + 1) * 512].rearrange("(k p) n -> p k n", p=128)))
                for blk in range(4):
                    n_ += 1
                    pp = pd_[n_ % 4]
                    for k in range(43):
                        S.mm(pp[:, :], hT[:, k, blk * 128:(blk + 1) * 128], w[:, k, :], start=(k == 0), stop=(k == 42))
                    x_ = xo[n_ % 4]
                    S.dma("sp", x_[:, 512:1024], x1_s[t0 + blk * 128:t0 + (blk + 1) * 128, cb * 512:(cb + 1) * 512])
                    S.tt("dve", x_[:, 0:512], pp[:, :], g2row[:, cb * 512:(cb + 1) * 512], ALU.mult)
                    S.tt("pool", x_[:, 0:512], x_[:, 0:512], x_[:, 512:1024], ALU.add)
                    S.dma("pool", out_d[t0 + blk * 128:t0 + (blk + 1) * 128, cb * 512:(cb + 1) * 512], x_[:, 0:512])
        P.end()
    return nc, S


DBG_OUT = set()


def _consts():
    n = 16
    inv = 10000.0 ** (-np.arange(n, dtype=np.float32) / n)
    t = np.arange(T)
    row = (t // 64).astype(np.float32)
    col = (t % 64).astype(np.float32)
    ang_r = row[:, None] * inv[None, :]
    ang_c = col[:, None] * inv[None, :]
    cos = np.concatenate([np.cos(ang_r), np.cos(ang_r), np.cos(ang_c), np.cos(ang_c)], axis=1).astype(np.float32)
    sin = np.concatenate([np.sin(ang_r), np.sin(ang_r), np.sin(ang_c), np.sin(ang_c)], axis=1).astype(np.float32)
    cos_t = np.concatenate([np.ones((TC, 64), np.float32), cos], axis=0).T.copy()
    sin_t = np.concatenate([np.zeros((TC, 64), np.float32), sin], axis=0).T.copy()
    Pm = np.zeros((64, 64), np.float32)
    for base in (0, 32):
        for i in range(16):
            Pm[base + i, base + 16 + i] = -1.0
            Pm[base + 16 + i, base + i] = 1.0
    protT = Pm.T.copy()
    k = np.arange(64)
    masks = np.zeros((64, 2, 4, 64), np.float32)
    masks[:, 0, 0, :] = (k[:, None] <= k[None, :])
    masks[:, 0, 1, :] = (k[:, None] > k[None, :])
    masks[:, 0, 2, :] = (k[:, None] > k[None, :])
    masks[:, 0, 3, :] = (k[:, None] >= k[None, :])
    masks[:, 1, 0, :] = (k[:, None] >= k[None, :])
    masks[:, 1, 1, :] = (k[:, None] < k[None, :])
    masks[:, 1, 2, :] = (k[:, None] < k[None, :])
    masks[:, 1, 3, :] = (k[:, None] <= k[None, :])
    return dict(identf=np.eye(128, dtype=np.float32), cos_t=cos_t, sin_t=sin_t, protT=protT, dn_masks=masks)


def make_in_maps(inp):
    c = _consts()
    f = lambda a: np.ascontiguousarray(np.asarray(a, dtype=np.float32))
    shared = dict(
        w_ada=f(inp["w_ada"][0]), b_ada=f(inp["b_ada"][0]), norm_mix=f(inp["norm_mix"][0]), norm_ffn=f(inp["norm_ffn"][0]),
        w_in=f(inp["w_in"][0]), q_a_norm=f(inp["q_a_norm"][0]), kv_a_norm=f(inp["kv_a_norm"][0]), w_q_b=f(inp["w_q_b"][0]),
        w_kv_b=f(inp["w_kv_b"][0]), q_norm=f(inp["q_norm"][0]), k_norm=f(inp["k_norm"][0]), w_o_mla=f(inp["w_o_mla"][0]),
        dn_conv=f(np.asarray(inp["dn_conv"][0]).reshape(3, 24, 128).transpose(2, 1, 0)), a_log_rep=f(np.broadcast_to(np.asarray(inp["dn_a_log"][0]).reshape(1, 16), (128, 16))),
        dt_bias_rep=f(np.broadcast_to(np.asarray(inp["dn_dt_bias"][0]).reshape(1, 16), (128, 16))),
        dn_o_norm=f(inp["dn_o_norm"][0]), w_o_dn=f(inp["w_o_dn"][0]), w_out=f(inp["w_out"][0]), w_ffn_up=f(inp["w_ffn_up"][0]),
        ffn_conv=f(np.asarray(inp["ffn_conv"][0]).reshape(3, 43, 128).transpose(2, 1, 0)), w_ffn_down=f(inp["w_ffn_down"][0]), **c)
    maps = []
    for core in range(NCORES):
        b = core % 2
        m = dict(shared)
        m["x"] = f(inp["x"][b])
        m["ctx"] = f(inp["ctx"][b])
        m["cc"] = f(np.stack([np.asarray(inp["c"][b]), np.asarray(inp["c_ctx"])], axis=1))
        maps.append(m)
    return maps


def kernel(**inp):
    nc, S = build_program()
    maps = make_in_maps(inp)[:NCORES]
    res = run_bass_kernel_spmd(nc, maps, core_ids=list(range(NCORES)))
    out = np.stack([np.asarray(res.results[0]["out"]), np.asarray(res.results[1]["out"])], axis=0)
    return out.astype(np.float32)
```
